# Optimizing a Trainium2 kernel written in Bass

```python
import jax, jax.numpy as jnp
from jax import lax
import numpy as np

D_MODEL = 1024
BATCH = 8
SEQ = 4096
DEPTH = 1

MLA_HEADS = 4
QK_NOPE = 128
QK_ROPE = 64
V_HEAD = 128
Q_LORA = 256
KV_LORA = 256
QK_HEAD = QK_NOPE + QK_ROPE
MLA_WIDTH = MLA_HEADS * V_HEAD
HG_HEADS = 4
HG_DK = 128
HG_DV = 128
HG_FDIM = HG_HEADS * HG_DK
HG_WIDTH = HG_HEADS * HG_DV
CHUNK = 64
D_MIX = MLA_WIDTH + HG_WIDTH
D_FF = -(-8 * D_MODEL // (3 * 256)) * 256
PLE_DIM = 256
ROPE_THETA = 10000.0
EPS = 1e-6
Q_BLOCK = 128
IN_SIZES = (Q_LORA, KV_LORA, QK_ROPE, HG_FDIM, HG_FDIM, HG_FDIM, HG_WIDTH, HG_WIDTH)
D_IN = sum(IN_SIZES)
IN_SPLITS = tuple(int(v) for v in np.cumsum(IN_SIZES)[:-1])

kernel_name = "hymba_mla_hgrn2_ple_block"


def rms_norm(x, g):
    xf = x.astype(jnp.float32)
    y = xf * lax.rsqrt(jnp.mean(xf * xf, axis=-1, keepdims=True) + EPS)
    return (y * g.astype(jnp.float32)).astype(x.dtype)


def rope(x, cos, sin):
    x1, x2 = jnp.split(x.astype(jnp.float32), 2, axis=-1)
    return jnp.concatenate([x1 * cos - x2 * sin, x1 * sin + x2 * cos], axis=-1).astype(x.dtype)


def blocked_attention(q, k, v):
    B, S, H, D = q.shape
    nb = S // Q_BLOCK
    qb = q.reshape(B, nb, Q_BLOCK, H, D).transpose(1, 0, 2, 3, 4)
    scale = QK_HEAD ** -0.5

    def one_block(q_blk):
        s = jnp.einsum('bqhd,bkhd->bhqk', q_blk, k, preferred_element_type=jnp.float32) * scale
        w = jax.nn.softmax(s, axis=-1).astype(v.dtype)
        return jnp.einsum('bhqk,bkhd->bqhd', w, v)

    o = lax.map(one_block, qb)
    return o.transpose(1, 0, 2, 3, 4).reshape(B, S, H * v.shape[-1])


def mla(c_q, c_kv, k_r, cos, sin, g_qa, g_kva, w_qb, w_kvb, g_qn, g_kn):
    B, S, _ = c_q.shape
    q = (rms_norm(c_q, g_qa) @ w_qb).reshape(B, S, MLA_HEADS, QK_HEAD)
    kv = (rms_norm(c_kv, g_kva) @ w_kvb).reshape(B, S, MLA_HEADS, QK_NOPE + V_HEAD)
    k_nope, v = kv[..., :QK_NOPE], kv[..., QK_NOPE:]
    k = jnp.concatenate([k_nope, jnp.broadcast_to(k_r[:, :, None, :], (B, S, MLA_HEADS, QK_ROPE))], axis=-1)
    q = rms_norm(q, g_qn)
    k = rms_norm(k, g_kn)
    c, s = cos[:, :, None, :], sin[:, :, None, :]
    q = jnp.concatenate([q[..., :QK_NOPE], rope(q[..., QK_NOPE:], c, s)], axis=-1)
    k = jnp.concatenate([k[..., :QK_NOPE], rope(k[..., QK_NOPE:], c, s)], axis=-1)
    return blocked_attention(q, k, v)


def gla_chunkwise(q, k, v, logf):
    B, H, S, DK = q.shape
    DV = v.shape[-1]
    N = S // CHUNK
    q = q.reshape(B, H, N, CHUNK, DK)
    k = k.reshape(B, H, N, CHUNK, DK)
    v = v.reshape(B, H, N, CHUNK, DV)
    b = jnp.cumsum(logf.reshape(B, H, N, CHUNK, DK), axis=3)
    b_last = b[:, :, :, -1:, :]
    b_mid = b[:, :, :, CHUNK // 2 - 1:CHUNK // 2, :]
    q_intra = q * jnp.exp(b - b_mid)
    k_intra = k * jnp.exp(b_mid - b)
    mask = jnp.tril(jnp.ones((CHUNK, CHUNK), jnp.float32))
    A = jnp.einsum('bhntd,bhnsd->bhnts', q_intra, k_intra) * mask
    o_intra = jnp.einsum('bhnts,bhnsv->bhntv', A, v)
    kv = jnp.einsum('bhnsd,bhnsv->bhndv', k * jnp.exp(b_last - b), v)
    decay = jnp.exp(b_last[:, :, :, 0, :])

    def step(state, inp):
        dec, kv_n = inp
        return dec[..., None] * state + kv_n, state

    _, s_before = lax.scan(step, jnp.zeros((B, H, DK, DV), jnp.float32),
                           (jnp.moveaxis(decay, 2, 0), jnp.moveaxis(kv, 2, 0)))
    s_before = jnp.moveaxis(s_before, 0, 2)
    o_inter = jnp.einsum('bhntd,bhndv->bhntv', q * jnp.exp(b), s_before)
    return (o_intra + o_inter).reshape(B, H, S, DV)


def hgrn2(hq, hf_fwd, hf_bwd, hi, hg, lb, g_out):
    B, S, _ = hq.shape

    def heads(t):
        return t.reshape(B, S, HG_HEADS, -1).transpose(0, 2, 1, 3)

    q = heads(jax.nn.silu(hq.astype(jnp.float32)))
    v = heads(hi.astype(jnp.float32))

    def gates(f_pre, lower):
        f = lower + (1.0 - lower) * jax.nn.sigmoid(f_pre.astype(jnp.float32))
        return heads(1.0 - f), heads(jnp.log(f))

    k_f, lf_f = gates(hf_fwd, lb[0])
    k_b, lf_b = gates(hf_bwd, lb[1])
    o_f = gla_chunkwise(q, k_f, v, lf_f)
    o_b = gla_chunkwise(q[:, :, ::-1], k_b[:, :, ::-1], v[:, :, ::-1], lf_b[:, :, ::-1])[:, :, ::-1]
    o = (o_f + o_b).transpose(0, 2, 1, 3)
    o = rms_norm(o, g_out).reshape(B, S, HG_WIDTH)
    return (o * jax.nn.silu(hg.astype(jnp.float32))).astype(hq.dtype)


def setup_inputs(seed: int = 0) -> dict:
    key = jax.random.key(seed)
    ks = jax.random.split(key, 32)
    f32 = jnp.float32

    def w(k, shape, fan_in):
        return jax.random.normal(k, shape, f32) * (fan_in ** -0.5)

    def gain(k, shape):
        return 1.0 + 0.05 * jax.random.normal(k, shape, f32)

    x = jax.random.normal(ks[0], (BATCH, SEQ, D_MODEL), f32)
    p = jax.random.normal(ks[1], (DEPTH, BATCH, SEQ, PLE_DIM), f32)
    offsets = jax.random.randint(ks[2], (BATCH, 1), 0, 1024, jnp.int32)
    positions = offsets + jnp.arange(SEQ, dtype=jnp.int32)[None, :]
    return {
        "x": x,
        "p": p,
        "positions": positions,
        "g_mix": gain(ks[3], (DEPTH, D_MODEL)),
        "w_in": w(ks[4], (DEPTH, D_MODEL, D_IN), D_MODEL),
        "g_qa": gain(ks[5], (DEPTH, Q_LORA)),
        "g_kva": gain(ks[6], (DEPTH, KV_LORA)),
        "w_qb": w(ks[7], (DEPTH, Q_LORA, MLA_HEADS * QK_HEAD), Q_LORA),
        "w_kvb": w(ks[8], (DEPTH, KV_LORA, MLA_HEADS * (QK_NOPE + V_HEAD)), KV_LORA),
        "g_qn": gain(ks[9], (DEPTH, QK_HEAD)),
        "g_kn": gain(ks[10], (DEPTH, QK_HEAD)),
        "lb_param": 0.1 * jax.random.normal(ks[11], (DEPTH + 1, 2, HG_FDIM), f32),
        "g_hgo": gain(ks[12], (DEPTH, HG_HEADS, HG_DV)),
        "w_o": w(ks[13], (DEPTH, D_MIX, D_MODEL), D_MIX),
        "g_ffn": gain(ks[14], (DEPTH, D_MODEL)),
        "w_gate": w(ks[15], (DEPTH, D_MODEL, D_FF), D_MODEL),
        "w_up": w(ks[16], (DEPTH, D_MODEL, D_FF), D_MODEL),
        "w_down": w(ks[17], (DEPTH, D_FF, D_MODEL), D_FF),
        "g_ple": gain(ks[18], (DEPTH, D_MODEL)),
        "w_ple_gate": w(ks[19], (DEPTH, D_MODEL, D_MODEL), D_MODEL),
        "w_ple_proj": w(ks[20], (DEPTH, PLE_DIM, D_MODEL), PLE_DIM),
    }


def reference(x, p, positions, g_mix, w_in, g_qa, g_kva, w_qb, w_kvb, g_qn, g_kn,
              lb_param, g_hgo, w_o, g_ffn, w_gate, w_up, w_down, g_ple, w_ple_gate, w_ple_proj):
    inv_freq = ROPE_THETA ** (-jnp.arange(0, QK_ROPE, 2, dtype=jnp.float32) / QK_ROPE)
    ang = positions.astype(jnp.float32)[..., None] * inv_freq
    cos, sin = jnp.cos(ang), jnp.sin(ang)
    lower_bounds = jnp.cumsum(jax.nn.softmax(lb_param.astype(jnp.float32), axis=0), axis=0)

    for l in range(DEPTH):
        h = rms_norm(x, g_mix[l])
        z = h @ w_in[l]
        c_q, c_kv, k_r, hq, hf_f, hf_b, hi, hg = jnp.split(z, IN_SPLITS, axis=-1)
        a = mla(c_q, c_kv, k_r, cos, sin, g_qa[l], g_kva[l], w_qb[l], w_kvb[l], g_qn[l], g_kn[l])
        r = hgrn2(hq, hf_f, hf_b, hi, hg, lower_bounds[l], g_hgo[l])
        x = x + jnp.concatenate([a, r], axis=-1) @ w_o[l]
        h = rms_norm(x, g_ffn[l])
        x = x + (jax.nn.silu(h @ w_gate[l]) * (h @ w_up[l])) @ w_down[l]
        gate = jax.nn.sigmoid(rms_norm(x, g_ple[l]) @ w_ple_gate[l])
        x = x + gate * (p[l].astype(x.dtype) @ w_ple_proj[l])
    return x
```

```python
import os
import types
import numpy as np
from contextlib import ExitStack
import concourse.bass as bass
import concourse.mybir as mybir
from concourse.bass_utils import run_bass_kernel_spmd

F32 = mybir.dt.float32
BF16 = mybir.dt.bfloat16
I32 = mybir.dt.int32
AF = mybir.ActivationFunctionType
ALU = mybir.AluOpType

T = 4096
DM = 1024
NT = 8
EPS = 1e-6
DFF = 2816
ZC = dict(cq=0, ckv=256, kr=512, hq=576, hff=1088, hfb=1600, hi=2112, hg=2624, krot=3136)
ZW = 3200
DEBUG = os.environ.get("MK_DEBUG", "")
STOP = os.environ.get("MK_STOP", "")


def _freeze(fn):
    if fn is None or fn.__closure__ is None:
        return fn
    cells = []
    for c in fn.__closure__:
        try:
            cells.append(types.CellType(c.cell_contents))
        except ValueError:
            cells.append(c)
    g = types.FunctionType(fn.__code__, fn.__globals__, fn.__name__, fn.__defaults__, tuple(cells))
    g.__kwdefaults__ = fn.__kwdefaults__
    return g


class Tl:
    __slots__ = ("name", "w", "r", "psum")

    def __init__(self, name):
        self.name = name
        self.w = {}
        self.r = {}
        self.psum = False


class Prog:
    ENG = ("pe", "act", "dve", "pool", "sp")

    def __init__(self, nc, es):
        self.nc = nc
        self.es = es
        self.ops = {e: [] for e in self.ENG}
        self.cnt = {e: 0 for e in self.ENG}
        self.dcnt = {}
        self.sems = {}
        self.final_tiles = []

    def tl(self, name):
        return Tl(name)

    def tls(self, name, n):
        return [Tl(f"{name}{i}") for i in range(n)]

    def op(self, eng, fn, r=(), w=(), dma=None):
        deps = {}

        def add(k, i):
            if deps.get(k, 0) < i:
                deps[k] = i
        for t in r:
            for k, i in t.w.items():
                add(k, i)
            if t.psum:
                for k, i in t.r.items():
                    if k != eng:
                        add(k, i)
        for t in w:
            for k, i in t.w.items():
                add(k, i)
            for k, i in t.r.items():
                add(k, i)
        if eng == "pe" and dma is None:
            deps.pop("pe", None)
        if dma is not None:
            key = "d_" + dma
            self.dcnt[key] = self.dcnt.get(key, 0) + 1
            ev = (key, self.dcnt[key])
        else:
            self.cnt[eng] += 1
            ev = (eng, self.cnt[eng])
        self.ops[eng].append([_freeze(fn), deps, ev, False])
        for t in r:
            if t.r.get(ev[0], 0) < ev[1]:
                t.r[ev[0]] = ev[1]
        for t in w:
            t.w[ev[0]] = ev[1]
        return ev

    def barrier(self):
        allev = {e: self.cnt[e] for e in self.ENG if self.cnt[e] > 0}
        allev.update(self.dcnt)
        for e in self.ENG:
            deps = dict(allev)
            self.cnt[e] += 1
            self.ops[e].append([None, deps, (e, self.cnt[e]), False])

    def emit(self):
        nc = self.nc
        index = {e: {} for e in self.ENG}
        for e in self.ENG:
            for o in self.ops[e]:
                if not o[2][0].startswith("d_"):
                    index[e][o[2][1]] = o
        for e in self.ENG:
            seen = {}
            for o in self.ops[e]:
                nd = {}
                for k, i in o[1].items():
                    if seen.get(k, 0) >= i:
                        continue
                    seen[k] = i
                    nd[k] = i
                    if not k.startswith("d_"):
                        index[k][i][3] = True
                o[1] = nd
        tick = {e: {} for e in self.ENG}
        for e in self.ENG:
            c = 0
            for o in self.ops[e]:
                if o[2][0].startswith("d_"):
                    continue
                if o[3]:
                    c += 1
                tick[e][o[2][1]] = c
        keys = set(self.dcnt) | set(self.ENG)
        for k in sorted(keys):
            self.sems[k] = self.es.enter_context(nc.semaphore("s_" + k))
        block = nc.Block()
        block.__enter__()
        engmap = dict(pe=block.tensor, act=block.scalar, dve=block.vector, pool=block.gpsimd, sp=block.sync)

        def make(e):
            def body(eng):
                for fn, deps, ev, need in self.ops[e]:
                    for k, i in deps.items():
                        v = 16 * i if k.startswith("d_") else tick[k][i]
                        eng.wait_ge(self.sems[k], v)
                    if fn is None:
                        if need:
                            eng.nop().then_inc(self.sems[ev[0]], 1)
                        continue
                    ins = fn(eng)
                    if ev[0].startswith("d_"):
                        ins.then_inc(self.sems[ev[0]], 16)
                    elif need:
                        ins.then_inc(self.sems[ev[0]], 1)
            return body
        for e in self.ENG:
            engmap[e](make(e))
        block.__exit__(None, None, None)


def build_program(debug_outs=()):
    nc = bass.Bass("TRN2", target_bir_lowering=False)
    es = ExitStack()
    P = Prog(nc, es)

    def din(name, shape, dt=F32):
        return nc.dram_tensor(name, list(shape), dt, kind="ExternalInput").ap()

    def dscr(name, shape, dt):
        kind = "ExternalOutput" if name in debug_outs else "Internal"
        return nc.dram_tensor(name, list(shape), dt, kind=kind).ap()

    x_d = din("x", [T, DM])
    p_d = din("p", [T, 256]) if STOP == "" else None
    pos_d = din("pos", [64, T], I32)
    win_d = din("w_in", [DM, ZW])
    wqb_d = din("w_qb", [256, 1024])
    wkvb_d = din("w_kvb", [256, 1024])
    wo_d = din("w_o", [DM, DM])
    wg_d = din("w_gate", [DM, DFF])
    wu_d = din("w_up", [DM, DFF])
    wd_d = din("w_down", [DFF, DM])
    wpg_d = din("w_pg", [DM, DM])
    wpp_d = din("w_pp", [256, DM])
    sm_d = din("smalls", [128, 64])
    lbtm_d = din("lb_tm", [128, 4, 512])
    ghg_d = din("ghg_tm", [128, 512])
    cst_d = din("consts", [128, 1152])

    out_d = nc.dram_tensor("out", [T, DM], F32, kind="ExternalOutput").ap()

    QTn = dscr("QTn", [4, 128, T], BF16)
    QTr = dscr("QTr", [4, 64, T], BF16)
    KTn = dscr("KTn", [4, 128, T], BF16)
    KTr = dscr("KTr", [4, 64, T], BF16)
    Vs = dscr("Vs", [4, 128, 32, 128], BF16)
    qTs = dscr("qTs", [4, 128, T], BF16)
    kTs = dscr("kTs", [2, 4, 128, T], BF16)
    lfs = dscr("lfs", [2, 4, 128, 32, 128], F32)
    ktms = dscr("ktms", [2, 4, 128, 32, 128], BF16)
    vhs = dscr("vhs", [4, 128, 32, 128], BF16)
    ggs = dscr("ggs", [4, 128, 32, 128], BF16)
    mixT = dscr("mixT", [8, 128, T], BF16)
    wo_s = dscr("wo_s", [DM, DM], BF16)
    wg_s = dscr("wg_s", [DM, DFF], BF16)
    wu_s = dscr("wu_s", [DM, DFF], BF16)
    wd_s = dscr("wd_s", [DFF, DM], BF16)
    wpg_s = dscr("wpg_s", [DM, DM], BF16)
    wpp_s = dscr("wpp_s", [256, DM], BF16)
    scr_t = {n: P.tl("scr_" + n) for n in
             ["QTn", "QTr", "KTn", "KTr", "Vs", "qTs", "kTs", "lfs", "ktms", "vhs", "ggs", "mixT", "wD"]}

    def sb(name, shape, dt):
        return es.enter_context(nc.sbuf_tensor("sb_" + name, list(shape), dt))

    def ps(name, shape, dt):
        return es.enter_context(nc.psum_tensor(name, list(shape), dt))

    smalls = sb("smalls", [128, 64], F32)
    t_smalls = P.tl("smalls")
    cst32 = sb("cst32", [128, 1152], F32)
    cstbf = sb("cstbf", [128, 1152], BF16)
    t_cst = P.tl("cst")
    onesbf = sb("onesbf", [128, 128], BF16)
    ones32 = sb("ones32", [128, 128], F32)
    t_ones = P.tl("ones")
    P.zt = sb("zt", [128, 1024], F32) if STOP else None
    lbfm = sb("lbfm", [128, 32], F32)
    t_lbfm = P.tl("lbfm")

    NPS = 6
    psb = [ps(f"psb{i}", [128, 512], F32) for i in range(NPS)]
    t_psb = P.tls("psb", NPS)
    pst = [ps(f"pst{i}", [128, 1024], BF16) for i in range(2)]
    t_pst = P.tls("pst", 2)
    for t_ in t_psb + t_pst:
        t_.psum = True
    st = dict(ps=0, pt=0)

    def getps():
        i = st["ps"] % NPS
        st["ps"] += 1
        return psb[i], t_psb[i]

    def getpt():
        i = st["pt"] % 2
        st["pt"] += 1
        return pst[i][:, 0:512], t_pst[i]

    def act_pow(dst, src, power, r, w, lntmp=None, t_ln=None):
        l = dst if lntmp is None else lntmp
        tl_l = list(w) if lntmp is None else [t_ln]
        P.op("act", lambda e: e.activation(out=l, in_=src, func=AF.Ln), r=list(r), w=tl_l)
        P.op("act", lambda e: e.activation(out=dst, in_=l, func=AF.Exp, scale=float(power)), r=tl_l, w=list(w))

    SM = dict(gmix=0, gffn=8, gple=16, gqa=24, gkva=26, gqn_n=28, gqn_r=29, gqn_rot=30, gkn_n=31, gkn_r=32,
              gkn_rot=33, invf=34, sgn=35, lbp=36)
    IDN, CMF, SUF, MKF, CMB, SUB, MKB = 0, 128, 384, 512, 640, 896, 1024

    P.op("sp", lambda e: e.dma_start(out=smalls[:], in_=sm_d[:, :]), w=[t_smalls], dma="c0")
    P.op("sp", lambda e: e.dma_start(out=cst32[:], in_=cst_d[:, :]), w=[t_cst], dma="c1")
    P.op("dve", lambda e: e.tensor_copy(cstbf[:], cst32[:]), r=[t_cst], w=[t_cst])
    P.op("dve", lambda e: e.memset(onesbf[:], 1.0), w=[t_ones])
    P.op("dve", lambda e: e.memset(ones32[:], 1.0), w=[t_ones])
    c = SM["lbp"]
    P.op("dve", lambda e: e.tensor_sub(lbfm[:, 8:16], smalls[:, c + 8:c + 16], smalls[:, c:c + 8]),
         r=[t_smalls], w=[t_lbfm])
    P.op("act", lambda e: e.activation(out=lbfm[:, 0:8], in_=lbfm[:, 8:16], func=AF.Sigmoid),
         r=[t_lbfm], w=[t_lbfm])
    for cc in (SM["gqn_rot"], SM["gkn_rot"]):
        P.op("dve", lambda e, cc=cc: e.tensor_mul(smalls[:, cc:cc + 1], smalls[:, cc:cc + 1],
                                                   smalls[:, SM["sgn"]:SM["sgn"] + 1]),
             r=[t_smalls], w=[t_smalls])

    ident = cstbf[:, IDN:IDN + 128]
    if STOP == "0":
        P.barrier()
        return finish(nc, P, es, out_d)

    def prep_weight(src, rows, cols, gcol, dst_fn, dst_tl, stg32, stgbf, t32, tbf, to_dram, key):
        nrc = rows // 128
        for rc in range(nrc):
            s = rc % 2
            P.op("pool", lambda e, rc=rc, s=s: e.dma_start(out=stg32[s][:, 0:cols], in_=src[rc * 128:(rc + 1) * 128, :]),
                 w=[t32[s]], dma=f"wl{s}")
            dstap = dst_fn(rc) if not to_dram else stgbf[s][:, 0:cols]
            wl = [dst_tl] if not to_dram else [tbf[s]]
            eng = "act" if rc % 2 == 0 else "dve"
            if gcol is None:
                if eng == "act":
                    P.op("act", lambda e, s=s, d=dstap: e.activation(out=d, in_=stg32[s][:, 0:cols], func=AF.Copy),
                         r=[t32[s]], w=wl)
                else:
                    P.op("dve", lambda e, s=s, d=dstap: e.tensor_copy(d, stg32[s][:, 0:cols]), r=[t32[s]], w=wl)
            else:
                g = smalls[:, gcol + rc:gcol + rc + 1]
                if eng == "act":
                    P.op("act", lambda e, s=s, d=dstap, g=g: e.activation(out=d, in_=stg32[s][:, 0:cols],
                                                                         func=AF.Copy, scale=g),
                         r=[t32[s], t_smalls], w=wl)
                else:
                    P.op("dve", lambda e, s=s, d=dstap, g=g: e.tensor_scalar(d, stg32[s][:, 0:cols], g, None,
                                                                            op0=ALU.mult),
                         r=[t32[s], t_smalls], w=wl)
            if to_dram:
                P.op("sp", lambda e, s=s, rc=rc: e.dma_start(out=dst_fn(rc), in_=stgbf[s][:, 0:cols]),
                     r=[tbf[s]], w=[dst_tl], dma=f"ws{s}")

    with ExitStack() as esA:
        def sbA(name, shape, dt):
            return esA.enter_context(nc.sbuf_tensor("a_" + name, list(shape), dt))
        win_sb = sbA("win_sb", [128, 8, ZW], BF16)
        t_win = P.tl("win")
        wq_sb = sbA("wq_sb", [128, 2, 1024], BF16)
        t_wq = P.tl("wq")
        wkv_sb = sbA("wkv_sb", [128, 2, 1024], BF16)
        t_wkv = P.tl("wkv")
        esW = ExitStack()
        stg32 = [esW.enter_context(nc.sbuf_tensor(f"wst32_{i}", [128, ZW], F32)) for i in range(2)]
        stgbf = [esW.enter_context(nc.sbuf_tensor(f"wstbf_{i}", [128, ZW], BF16)) for i in range(2)]
        t32 = P.tls("wst32_", 2)
        tbf = P.tls("wstbf_", 2)

        prep_weight(win_d, DM, ZW, SM["gmix"], lambda rc: win_sb[:, rc, :], t_win, stg32, stgbf, t32, tbf, False, "win")
        if STOP == "W1":
            P.barrier()
            esW.close()
            return finish(nc, P, es, out_d)
        prep_weight(wqb_d, 256, 1024, SM["gqa"], lambda rc: wq_sb[:, rc, :], t_wq, stg32, stgbf, t32, tbf, False, "wq")
        prep_weight(wkvb_d, 256, 1024, SM["gkva"], lambda rc: wkv_sb[:, rc, :], t_wkv, stg32, stgbf, t32, tbf, False, "wkv")
        tw = scr_t["wD"]
        prep_weight(wo_d, DM, DM, None, lambda rc: wo_s[rc * 128:(rc + 1) * 128, :], tw, stg32, stgbf, t32, tbf, True, "wo")
        prep_weight(wg_d, DM, DFF, SM["gffn"], lambda rc: wg_s[rc * 128:(rc + 1) * 128, :], tw, stg32, stgbf, t32, tbf, True, "wg")
        prep_weight(wu_d, DM, DFF, SM["gffn"], lambda rc: wu_s[rc * 128:(rc + 1) * 128, :], tw, stg32, stgbf, t32, tbf, True, "wu")
        prep_weight(wd_d, DFF, DM, None, lambda rc: wd_s[rc * 128:(rc + 1) * 128, :], tw, stg32, stgbf, t32, tbf, True, "wd")
        prep_weight(wpg_d, DM, DM, SM["gple"], lambda rc: wpg_s[rc * 128:(rc + 1) * 128, :], tw, stg32, stgbf, t32, tbf, True, "wpg")
        prep_weight(wpp_d, 256, DM, None, lambda rc: wpp_s[rc * 128:(rc + 1) * 128, :], tw, stg32, stgbf, t32, tbf, True, "wpp")

        P.barrier()
        esW.close()
        if STOP == "W":
            return finish(nc, P, es, out_d)
        NXS = 2
        xs = [sbA(f"xs{i}", [128, DM], F32) for i in range(NXS)]
        t_xs = P.tls("xs", NXS)
        junk = sbA("junk", [128, DM], BF16)
        t_junk = P.tl("junk")
        hb = [sbA(f"hb{i}", [128, DM], BF16) for i in range(2)]
        t_hb = P.tls("hb", 2)
        hT = [sbA(f"hT{i}", [128, 8, 512], BF16) for i in range(2)]
        t_hT = P.tls("hT", 2)
        stat = sbA("stat", [128, 64], F32)
        t_stat = P.tls("stat", 64)
        lbtm = sbA("lbtm", [128, 2, 512], F32)
        t_lbtm = P.tl("lbtm")
        ghg = sbA("ghg", [128, 512], F32)
        t_ghg = P.tl("ghg")
        posf = sbA("posf", [64, 512], F32)
        posi = sbA("posi", [64, 512], I32)
        t_pos = P.tl("pos")
        cs = sbA("cs", [64, 2, 512], F32)
        t_cs = P.tl("cs")
        tmpA = [sbA(f"tmpA{i}", [128, 512], F32) for i in range(6)]
        t_tmpA = P.tls("tmpA", 6)
        sq = sbA("sq", [128, 2, 512], BF16)
        t_sq = P.tl("sq")
        cq_sb = sbA("cq_sb", [128, 2, 512], BF16)
        t_cq = P.tl("cq")
        ckv_sb = sbA("ckv_sb", [128, 2, 512], BF16)
        t_ckv = P.tl("ckv")
        bcs = sbA("bcs", [128, 6, 512], F32)
        t_bcs = P.tls("bcs", 6)
        KR = sbA("KR", [64, 512], F32)
        t_KR = P.tl("KR")
        S_QTn = sbA("S_QTn", [128, 4, 512], BF16)
        S_QTr = sbA("S_QTr", [64, 4, 512], BF16)
        S_KTn = sbA("S_KTn", [128, 4, 512], BF16)
        S_KTr = sbA("S_KTr", [64, 4, 512], BF16)
        S_V = sbA("S_V", [128, 4, 512], BF16)
        S_qT = sbA("S_qT", [128, 4, 512], BF16)
        S_kT = [sbA(f"S_kT{d}", [128, 4, 512], BF16) for d in range(2)]
        S_lf = [sbA(f"S_lf{d}", [128, 2, 512], F32) for d in range(2)]
        S_ktm = [sbA(f"S_ktm{d}", [128, 4, 512], BF16) for d in range(2)]
        S_vh = sbA("S_vh", [128, 4, 512], BF16)
        S_gg = sbA("S_gg", [128, 4, 512], BF16)
        tS = {n: P.tl("S_" + n) for n in ["QTn", "QTr", "KTn", "KTr", "V", "qT", "kT0", "kT1", "lf0", "lf1",
                                          "ktm0", "ktm1", "vh", "gg"]}

        P.op("sp", lambda e: e.dma_start(out=lbtm[:], in_=lbtm_d[:, 0:2, :]), w=[t_lbtm], dma="c2")
        P.op("sp", lambda e: e.dma_start(out=ghg[:], in_=ghg_d[:, :]), w=[t_ghg], dma="c3")
        for d in range(2):
            P.op("sp", lambda e, d=d: e.dma_start(out=tmpA[d][:], in_=lbtm_d[:, 2 + d, :]), w=[t_tmpA[d]], dma=f"c4_{d}")
            P.op("dve", lambda e, d=d: e.tensor_sub(tmpA[d][:], tmpA[d][:], lbtm[:, d, :]), r=[t_lbtm, t_tmpA[d]],
                 w=[t_tmpA[d]])
            P.op("act", lambda e, d=d: e.activation(out=lbtm[:, d, :], in_=tmpA[d][:], func=AF.Sigmoid),
                 r=[t_tmpA[d]], w=[t_lbtm])

        TWO_PI = float(2.0 * np.pi)

        def sincos(tt):
            P.op("sp", lambda e: e.dma_start(out=posi[:], in_=pos_d[:, tt * 512:(tt + 1) * 512]), w=[t_pos], dma="c5")
            P.op("dve", lambda e: e.tensor_copy(posf[:], posi[:]), r=[t_pos], w=[t_pos])
            P.op("dve", lambda e: e.tensor_scalar(posf[:], posf[:], smalls[0:64, SM["invf"]:SM["invf"] + 1], None,
                                                  op0=ALU.mult), r=[t_pos, t_smalls], w=[t_pos])
            a = posf[:, :]
            C1 = 6.28125
            C2 = float(2.0 * np.pi - 6.28125)
            PI_LO = 3.1415925
            for j, shift in ((1, 0.0), (0, float(np.pi / 2))):
                tA = tmpA[0][0:64, :]
                tB = tmpA[1][0:64, :]
                tI = tmpA[2][0:64, :].bitcast(I32)
                P.op("dve", lambda e, tA=tA, shift=shift: e.tensor_scalar(
                    tA, a, float(1.0 / TWO_PI), float(shift / TWO_PI), op0=ALU.mult, op1=ALU.add),
                    r=[t_pos], w=[t_tmpA[0]])
                P.op("dve", lambda e, tA=tA, tI=tI: e.tensor_copy(tI, tA), r=[t_tmpA[0]], w=[t_tmpA[2]])
                P.op("dve", lambda e, tA=tA, tI=tI: e.tensor_copy(tA, tI), r=[t_tmpA[2]], w=[t_tmpA[0]])
                P.op("dve", lambda e, tA=tA, tB=tB: e.scalar_tensor_tensor(tB, tA, -C1, a, op0=ALU.mult, op1=ALU.add),
                     r=[t_tmpA[0], t_pos], w=[t_tmpA[1]])
                P.op("dve", lambda e, tA=tA, tB=tB: e.scalar_tensor_tensor(tB, tA, -C2, tB, op0=ALU.mult, op1=ALU.add),
                     r=[t_tmpA[0], t_tmpA[1]], w=[t_tmpA[1]])
                P.op("dve", lambda e, tB=tB, shift=shift: e.tensor_scalar(tB, tB, float(shift), -PI_LO, op0=ALU.add,
                                                                           op1=ALU.max),
                     r=[t_tmpA[1]], w=[t_tmpA[1]])
                P.op("dve", lambda e, tB=tB: e.tensor_scalar(tB, tB, PI_LO, None, op0=ALU.min),
                     r=[t_tmpA[1]], w=[t_tmpA[1]])
                P.op("act", lambda e, tB=tB, j=j: e.activation(out=cs[:, j, :], in_=tB, func=AF.Sin),
                     r=[t_tmpA[1]], w=[t_cs])

        def colss(srcs, out_ps, t_out):
            n = len(srcs)
            for i, (ap, k, tl_) in enumerate(srcs):
                P.op("pe", lambda e, ap=ap, k=k, i=i: e.matmul(out_ps[:, :], onesbf[0:k, :], ap,
                                                                start=(i == 0), stop=(i == n - 1)),
                     r=[tl_, t_ones], w=[t_out])

        xcount = [0]
        for tt in range(int(os.environ.get("MK_NTA", NT))):
            sl = tt % 2
            for sub in range(4):
                xi = xcount[0] % NXS
                hi_ = xcount[0] % 2
                xcount[0] += 1
                r0 = tt * 512 + sub * 128
                P.op("pool", lambda e, xi=xi, r0=r0: e.dma_start(out=xs[xi][:], in_=x_d[r0:r0 + 128, :]),
                     w=[t_xs[xi]], dma=f"x{xi}")
                sc = stat[:, sub:sub + 1]
                P.op("act", lambda e, xi=xi, sc=sc: e.activation(out=junk[:], in_=xs[xi][:], func=AF.Square,
                                                                  accum_out=sc),
                     r=[t_xs[xi]], w=[t_junk, t_stat[sub]])
                P.op("dve", lambda e, sc=sc: e.tensor_scalar(sc, sc, 1.0 / DM, EPS, op0=ALU.mult, op1=ALU.add),
                     r=[t_stat[sub]], w=[t_stat[sub]])
                act_pow(sc, sc, -0.5, [t_stat[sub]], [t_stat[sub]])
                P.op("dve", lambda e, xi=xi, hi_=hi_, sc=sc: e.tensor_scalar(hb[hi_][:], xs[xi][:], sc, None,
                                                                            op0=ALU.mult),
                     r=[t_xs[xi], t_stat[sub]], w=[t_hb[hi_]])
                for half in range(2):
                    pt, tpt = getpt()
                    for j in range(4):
                        fc = half * 4 + j
                        P.op("pe", lambda e, pt=pt, hi_=hi_, fc=fc, j=j: e.transpose(
                            pt[:, j * 128:(j + 1) * 128], hb[hi_][:, fc * 128:(fc + 1) * 128], ident),
                            r=[t_hb[hi_], t_cst], w=[tpt])
                    eng = "act" if half == 0 else "dve"
                    dst = hT[sl][:, half * 4:half * 4 + 4, sub * 128:(sub + 1) * 128]
                    src = pt.rearrange("p (a b) -> p a b", b=128)
                    if eng == "act":
                        P.op("act", lambda e, dst=dst, src=src: e.activation(out=dst, in_=src, func=AF.Copy),
                             r=[tpt], w=[t_hT[sl]])
                    else:
                        P.op("dve", lambda e, dst=dst, src=src: e.tensor_copy(dst, src), r=[tpt], w=[t_hT[sl]])
            sincos(tt)

            def zfm(c0, width):
                pz, tpz = getps()
                for fc in range(8):
                    P.op("pe", lambda e, pz=pz, fc=fc: e.matmul(pz[0:width, :], win_sb[:, fc, c0:c0 + width],
                                                                hT[sl][:, fc, :], start=(fc == 0), stop=(fc == 7)),
                         r=[t_win, t_hT[sl]], w=[tpz])
                return pz, tpz

            def ztm(c0, sub):
                pz, tpz = getps()
                for fc in range(8):
                    P.op("pe", lambda e, pz=pz, fc=fc: e.matmul(pz[:, :], hT[sl][:, fc, sub * 128:(sub + 1) * 128],
                                                                win_sb[:, fc, c0:c0 + 512],
                                                                start=(fc == 0), stop=(fc == 7)),
                         r=[t_win, t_hT[sl]], w=[tpz])
                return pz, tpz

            for (name, csb, tcsb) in (("cq", cq_sb, t_cq), ("ckv", ckv_sb, t_ckv)):
                for j in range(2):
                    pz, tpz = zfm(ZC[name] + j * 128, 128)
                    P.op("act", lambda e, pz=pz, j=j: e.activation(out=sq[:, j, :], in_=pz[:, :], func=AF.Square),
                         r=[tpz], w=[t_sq])
                    P.op("dve", lambda e, pz=pz, j=j, csb=csb: e.tensor_copy(csb[:, j, :], pz[:, :]),
                         r=[tpz], w=[tcsb])
                pss, tpss = getps()
                colss([(sq[:, 0, :], 128, t_sq), (sq[:, 1, :], 128, t_sq)], pss, tpss)
                if name == "cq":
                    P.op("dve", lambda e, pss=pss: e.tensor_scalar(bcs[:, 0, :], pss[:, :], EPS / 256.0, EPS * EPS,
                                                                   op0=ALU.mult, op1=ALU.add),
                         r=[tpss], w=[t_bcs[0]])
                else:
                    P.op("dve", lambda e, pss=pss: e.tensor_scalar(bcs[:, 1, :], pss[:, :], 1.0 / 256.0, EPS,
                                                                   op0=ALU.mult, op1=ALU.add),
                         r=[tpss], w=[t_bcs[1]])
                    P.op("act", lambda e: e.activation(out=bcs[:, 5, :], in_=bcs[:, 1, :], func=AF.Ln),
                         r=[t_bcs[1]], w=[t_bcs[5]])
                    P.op("act", lambda e: e.activation(out=bcs[:, 2, :], in_=bcs[:, 5, :], func=AF.Exp, scale=-1.0),
                         r=[t_bcs[5]], w=[t_bcs[2]])
                    P.op("act", lambda e: e.activation(out=bcs[:, 1, :], in_=bcs[:, 5, :], func=AF.Exp, scale=-0.5),
                         r=[t_bcs[5]], w=[t_bcs[1]])
                    for sub in range(4):
                        pq, tpq = getps()
                        for kc in range(2):
                            P.op("pe", lambda e, pq=pq, kc=kc, sub=sub: e.matmul(
                                pq[:, 0:1], sq[:, kc, sub * 128:(sub + 1) * 128], onesbf[:, 0:1],
                                start=(kc == 0), stop=(kc == 1)), r=[t_sq, t_ones], w=[tpq])
                        sc = stat[:, 8 + sub:9 + sub]
                        P.op("dve", lambda e, pq=pq, sc=sc: e.tensor_scalar(sc, pq[:, 0:1], 1.0 / 256.0, EPS,
                                                                            op0=ALU.mult, op1=ALU.add),
                             r=[tpq], w=[t_stat[8 + sub]])
                        act_pow(sc, sc, -0.5, [t_stat[8 + sub]], [t_stat[8 + sub]])

            def mm2(wsb, twsb, c0, width, act_sb, tact):
                pz, tpz = getps()
                for kc in range(2):
                    P.op("pe", lambda e, pz=pz, kc=kc: e.matmul(pz[0:width, :], wsb[:, kc, c0:c0 + width],
                                                                act_sb[:, kc, :], start=(kc == 0), stop=(kc == 1)),
                         r=[twsb, tact], w=[tpz])
                return pz, tpz

            def gcol(n, rows=128):
                return smalls[0:rows, SM[n]:SM[n] + 1]

            def rope_combine(p1, tp1, p2, tp2, g1, g2, out_ap, t_out, scale_bc=None, t_scale=None):
                a = tmpA[2][0:64, :]
                b = tmpA[3][0:64, :]
                P.op("dve", lambda e: e.scalar_tensor_tensor(a, p1[0:64, :], g1, cs[:, 0, :], op0=ALU.mult, op1=ALU.mult),
                     r=[tp1, t_cs, t_smalls], w=[t_tmpA[2]])
                P.op("dve", lambda e: e.scalar_tensor_tensor(b, p2[0:64, :], g2, cs[:, 1, :], op0=ALU.mult, op1=ALU.mult),
                     r=[tp2, t_cs, t_smalls], w=[t_tmpA[3]])
                if scale_bc is None:
                    P.op("pool", lambda e: e.tensor_tensor(out_ap, a, b, op=ALU.add),
                         r=[t_tmpA[2], t_tmpA[3]], w=[t_out])
                else:
                    P.op("pool", lambda e: e.tensor_tensor(a, a, b, op=ALU.add),
                         r=[t_tmpA[2], t_tmpA[3]], w=[t_tmpA[2]])
                    P.op("dve", lambda e: e.tensor_tensor(out_ap, a, scale_bc, op=ALU.mult),
                         r=[t_tmpA[2], t_scale], w=[t_out])

            for h in range(4):
                pn, tpn = mm2(wq_sb, t_wq, h * 128, 128, cq_sb, t_cq)
                pr, tpr = mm2(wq_sb, t_wq, 512 + h * 64, 64, cq_sb, t_cq)
                pro, tpro = mm2(wq_sb, t_wq, 768 + h * 64, 64, cq_sb, t_cq)
                P.op("act", lambda e, pn=pn: e.activation(out=sq[:, 0, :], in_=pn[:, :], func=AF.Square),
                     r=[tpn], w=[t_sq])
                P.op("act", lambda e, pr=pr: e.activation(out=sq[0:64, 1, :], in_=pr[0:64, :], func=AF.Square),
                     r=[tpr], w=[t_sq])
                pss, tpss = getps()
                colss([(sq[:, 0, :], 128, t_sq), (sq[0:64, 1, :], 64, t_sq)], pss, tpss)
                P.op("dve", lambda e, pss=pss: e.scalar_tensor_tensor(bcs[:, 4, :], pss[:, :], 1.0 / 192.0, bcs[:, 0, :],
                                                                      op0=ALU.mult, op1=ALU.add),
                     r=[tpss, t_bcs[0]], w=[t_bcs[4]])
                act_pow(bcs[:, 4, :], bcs[:, 4, :], -0.5, [t_bcs[4]], [t_bcs[4]])
                P.op("dve", lambda e, pn=pn, h=h: e.scalar_tensor_tensor(S_QTn[:, h, :], pn[:, :], gcol("gqn_n"),
                                                                         bcs[:, 4, :], op0=ALU.mult, op1=ALU.mult),
                     r=[tpn, t_bcs[4], t_smalls], w=[tS["QTn"]])
                rope_combine(pr, tpr, pro, tpro, gcol("gqn_r", 64), gcol("gqn_rot", 64), S_QTr[:, h, :], tS["QTr"],
                             bcs[0:64, 4, :], t_bcs[4])
            P.op("sp", lambda e, tt=tt: e.dma_start(out=QTn[:, :, tt * 512:(tt + 1) * 512].rearrange("h d t -> d h t"),
                                                   in_=S_QTn[:]), r=[tS["QTn"]], w=[scr_t["QTn"]], dma="sQTn")
            P.op("sp", lambda e, tt=tt: e.dma_start(out=QTr[:, :, tt * 512:(tt + 1) * 512].rearrange("h d t -> d h t"),
                                                   in_=S_QTr[:]), r=[tS["QTr"]], w=[scr_t["QTr"]], dma="sQTr")

            pk1, tpk1 = zfm(ZC["kr"], 64)
            pk2, tpk2 = zfm(ZC["krot"], 64)
            P.op("act", lambda e: e.activation(out=sq[0:64, 1, :], in_=pk1[0:64, :], func=AF.Square),
                 r=[tpk1], w=[t_sq])
            pss, tpss = getps()
            colss([(sq[0:64, 1, :], 64, t_sq)], pss, tpss)
            P.op("act", lambda e, pss=pss: e.activation(out=bcs[:, 3, :], in_=pss[:, :], func=AF.Copy),
                 r=[tpss], w=[t_bcs[3]])
            rope_combine(pk1, tpk1, pk2, tpk2, gcol("gkn_r", 64), gcol("gkn_rot", 64), KR[:], t_KR)
            for h in range(4):
                pn, tpn = mm2(wkv_sb, t_wkv, h * 128, 128, ckv_sb, t_ckv)
                P.op("act", lambda e, pn=pn: e.activation(out=sq[:, 0, :], in_=pn[:, :], func=AF.Square),
                     r=[tpn], w=[t_sq])
                pss, tpss = getps()
                colss([(sq[:, 0, :], 128, t_sq)], pss, tpss)
                P.op("dve", lambda e, pss=pss: e.scalar_tensor_tensor(bcs[:, 4, :], pss[:, :], 1.0 / 192.0, bcs[:, 2, :],
                                                                      op0=ALU.mult, op1=ALU.mult),
                     r=[tpss, t_bcs[2]], w=[t_bcs[4]])
                P.op("dve", lambda e: e.scalar_tensor_tensor(bcs[:, 4, :], bcs[:, 3, :], 1.0 / 192.0, bcs[:, 4, :],
                                                             op0=ALU.mult, op1=ALU.add),
                     r=[t_bcs[3], t_bcs[4]], w=[t_bcs[4]])
                P.op("dve", lambda e: e.tensor_scalar(bcs[:, 4, :], bcs[:, 4, :], EPS, None, op0=ALU.add),
                     r=[t_bcs[4]], w=[t_bcs[4]])
                act_pow(bcs[:, 4, :], bcs[:, 4, :], -0.5, [t_bcs[4]], [t_bcs[4]])
                P.op("pool", lambda e: e.tensor_tensor(bcs[:, 5, :], bcs[:, 4, :], bcs[:, 1, :], op=ALU.mult),
                     r=[t_bcs[4], t_bcs[1]], w=[t_bcs[5]])
                P.op("dve", lambda e, pn=pn, h=h: e.scalar_tensor_tensor(S_KTn[:, h, :], pn[:, :], gcol("gkn_n"),
                                                                         bcs[:, 5, :], op0=ALU.mult, op1=ALU.mult),
                     r=[tpn, t_bcs[5], t_smalls], w=[tS["KTn"]])
                P.op("pool", lambda e, h=h: e.tensor_tensor(S_KTr[:, h, :], KR[:], bcs[0:64, 4, :], op=ALU.mult),
                     r=[t_KR, t_bcs[4]], w=[tS["KTr"]])
            P.op("sp", lambda e, tt=tt: e.dma_start(out=KTn[:, :, tt * 512:(tt + 1) * 512].rearrange("h d t -> d h t"),
                                                   in_=S_KTn[:]), r=[tS["KTn"]], w=[scr_t["KTn"]], dma="sKTn")
            P.op("sp", lambda e, tt=tt: e.dma_start(out=KTr[:, :, tt * 512:(tt + 1) * 512].rearrange("h d t -> d h t"),
                                                   in_=S_KTr[:]), r=[tS["KTr"]], w=[scr_t["KTr"]], dma="sKTr")

            for sub in range(4):
                pv, tpv = getps()
                for kc in range(2):
                    P.op("pe", lambda e, pv=pv, kc=kc, sub=sub: e.matmul(
                        pv[:, :], ckv_sb[:, kc, sub * 128:(sub + 1) * 128], wkv_sb[:, kc, 512:1024],
                        start=(kc == 0), stop=(kc == 1)), r=[t_ckv, t_wkv], w=[tpv])
                P.op("act", lambda e, pv=pv, sub=sub: e.activation(out=S_V[:, sub, :], in_=pv[:, :], func=AF.Copy,
                                                                    scale=stat[:, 8 + sub:9 + sub]),
                     r=[tpv, t_stat[8 + sub]], w=[tS["V"]])
            for h in range(4):
                P.op("sp", lambda e, tt=tt, h=h: e.dma_start(out=Vs[h, :, tt * 4:(tt + 1) * 4, :],
                                                            in_=S_V[:, :, h * 128:(h + 1) * 128]),
                     r=[tS["V"]], w=[scr_t["Vs"]], dma="sV")

            for h in range(4):
                pz, tpz = zfm(ZC["hq"] + h * 128, 128)
                P.op("act", lambda e, pz=pz, h=h: e.activation(out=S_qT[:, h, :], in_=pz[:, :], func=AF.Silu),
                     r=[tpz], w=[tS["qT"]])
            P.op("sp", lambda e, tt=tt: e.dma_start(out=qTs[:, :, tt * 512:(tt + 1) * 512].rearrange("h d t -> d h t"),
                                                   in_=S_qT[:]), r=[tS["qT"]], w=[scr_t["qTs"]], dma="sqT")
            for d, nm in ((0, "hff"), (1, "hfb")):
                for h in range(4):
                    pz, tpz = zfm(ZC[nm] + h * 128, 128)
                    tm = tmpA[4]
                    P.op("act", lambda e, pz=pz, tm=tm: e.activation(out=tm[:], in_=pz[:, :], func=AF.Sigmoid, scale=-1.0),
                         r=[tpz], w=[t_tmpA[4]])
                    oc = lbfm[:, d * 4 + h:d * 4 + h + 1]
                    P.op("dve", lambda e, tm=tm, d=d, h=h, oc=oc: e.tensor_scalar(S_kT[d][:, h, :], tm[:], oc, None,
                                                                                 op0=ALU.mult),
                         r=[t_tmpA[4], t_lbfm], w=[tS[f"kT{d}"]])
                P.op("sp", lambda e, tt=tt, d=d: e.dma_start(
                    out=kTs[d, :, :, tt * 512:(tt + 1) * 512].rearrange("h d t -> d h t"), in_=S_kT[d][:]),
                    r=[tS[f"kT{d}"]], w=[scr_t["kTs"]], dma=f"skT{d}")
            for d, nm in ((0, "hff"), (1, "hfb")):
                for sub in range(4):
                    pz, tpz = ztm(ZC[nm], sub)
                    tm = tmpA[4]
                    tk = tmpA[5]
                    P.op("act", lambda e, pz=pz, tm=tm: e.activation(out=tm[:], in_=pz[:, :], func=AF.Sigmoid, scale=-1.0),
                         r=[tpz], w=[t_tmpA[4]])
                    P.op("dve", lambda e, tm=tm, tk=tk, d=d: e.tensor_tensor(tk[:], tm[:], lbtm[:, d, :], op=ALU.mult),
                         r=[t_tmpA[4], t_lbtm], w=[t_tmpA[5]])
                    P.op("act", lambda e, tk=tk, d=d, sub=sub: e.activation(out=S_lf[d][:, sub % 2, :], in_=tk[:], func=AF.Ln,
                                                                             scale=-1.0, bias=1.0),
                         r=[t_tmpA[5]], w=[tS[f"lf{d}"]])
                    P.op("pool", lambda e, tk=tk, d=d, sub=sub: e.tensor_copy(S_ktm[d][:, sub, :], tk[:]),
                         r=[t_tmpA[5]], w=[tS[f"ktm{d}"]])
                    if sub % 2 == 1:
                        for h in range(4):
                            b0 = tt * 4 + sub - 1
                            P.op("sp", lambda e, b0=b0, d=d, h=h: e.dma_start(out=lfs[d, h, :, b0:b0 + 2, :],
                                                                             in_=S_lf[d][:, :, h * 128:(h + 1) * 128]),
                                 r=[tS[f"lf{d}"]], w=[scr_t["lfs"]], dma=f"slf{d}")
                for h in range(4):
                    P.op("sp", lambda e, tt=tt, d=d, h=h: e.dma_start(out=ktms[d, h, :, tt * 4:(tt + 1) * 4, :],
                                                                     in_=S_ktm[d][:, :, h * 128:(h + 1) * 128]),
                         r=[tS[f"ktm{d}"]], w=[scr_t["ktms"]], dma=f"sktm{d}")
            for sub in range(4):
                pz, tpz = ztm(ZC["hi"], sub)
                P.op("dve", lambda e, pz=pz, sub=sub: e.tensor_copy(S_vh[:, sub, :], pz[:, :]), r=[tpz], w=[tS["vh"]])
            for sub in range(4):
                pz, tpz = ztm(ZC["hg"], sub)
                tm = tmpA[4]
                P.op("act", lambda e, pz=pz, tm=tm: e.activation(out=tm[:], in_=pz[:, :], func=AF.Silu),
                     r=[tpz], w=[t_tmpA[4]])
                P.op("dve", lambda e, tm=tm, sub=sub: e.tensor_tensor(S_gg[:, sub, :], tm[:], ghg[:], op=ALU.mult),
                     r=[t_tmpA[4], t_ghg], w=[tS["gg"]])
            for h in range(4):
                P.op("sp", lambda e, tt=tt, h=h: e.dma_start(out=vhs[h, :, tt * 4:(tt + 1) * 4, :],
                                                            in_=S_vh[:, :, h * 128:(h + 1) * 128]),
                     r=[tS["vh"]], w=[scr_t["vhs"]], dma="svh")
                P.op("sp", lambda e, tt=tt, h=h: e.dma_start(out=ggs[h, :, tt * 4:(tt + 1) * 4, :],
                                                            in_=S_gg[:, :, h * 128:(h + 1) * 128]),
                     r=[tS["gg"]], w=[scr_t["ggs"]], dma="sgg")
        P.barrier()
    if STOP == "A":
        return finish(nc, P, es, out_d)

    SCALE = float(192.0 ** -0.5)
    with ExitStack() as esB:
        def sbB(name, shape, dt):
            return esB.enter_context(nc.sbuf_tensor("b_" + name, list(shape), dt))
        QN = [sbB(f"QN{i}", [128, T], BF16) for i in range(2)]
        QR = [sbB(f"QR{i}", [64, T], BF16) for i in range(2)]
        KN = [sbB(f"KN{i}", [128, T], BF16) for i in range(2)]
        KRr = [sbB(f"KR{i}", [64, T], BF16) for i in range(2)]
        VV = [sbB(f"VV{i}", [128, 32, 128], BF16) for i in range(2)]
        t_hd = P.tls("hd", 2)
        NPT = 4
        Pt = [sbB(f"Pt{i}", [128, 512], BF16) for i in range(NPT)]
        t_Pt = P.tls("Pt", NPT)
        rinv = sbB("rinv", [128, 512], F32)
        t_rinv = P.tl("rinv")
        Ost = [sbB(f"Ost{i}", [128, 512], BF16) for i in range(2)]
        t_Ost = P.tls("Ost", 2)

        def load_head(h):
            sl = h % 2
            for (dst, src) in ((QN[sl][:], QTn[h, :, :]), (QR[sl][:], QTr[h, :, :]), (KN[sl][:], KTn[h, :, :]),
                               (KRr[sl][:], KTr[h, :, :]), (VV[sl][:], Vs[h, :, :, :])):
                P.op("pool", lambda e, dst=dst, src=src: e.dma_start(out=dst, in_=src),
                     r=[scr_t["QTn"], scr_t["QTr"], scr_t["KTn"], scr_t["KTr"], scr_t["Vs"]], w=[t_hd[sl]],
                     dma=f"bl{sl}")

        steps = [(h, qb, kt) for h in range(4) for qb in range(8) for kt in range(32)]
        NS = len(steps)
        s_info = {}

        def issue_S(i):
            h, qb, kt = steps[i]
            sl = h % 2
            bi = i % 2
            pS, tpS = psb[bi], t_psb[bi]
            P.op("pe", lambda e: e.matmul(pS[:, :], KN[sl][:, kt * 128:(kt + 1) * 128], QN[sl][:, qb * 512:(qb + 1) * 512],
                                          start=True, stop=False), r=[t_hd[sl]], w=[tpS])
            P.op("pe", lambda e: e.matmul(pS[:, :], KRr[sl][0:64, kt * 128:(kt + 1) * 128],
                                          QR[sl][0:64, qb * 512:(qb + 1) * 512], start=False, stop=True),
                 r=[t_hd[sl]], w=[tpS])
            pi = i % NPT
            P.op("act", lambda e: e.activation(out=Pt[pi][:], in_=pS[:, :], func=AF.Exp, scale=SCALE),
                 r=[tpS], w=[t_Pt[pi]])

        load_head(0)
        load_head(1)
        issue_S(0)
        issue_S(1)
        for i in range(NS):
            h, qb, kt = steps[i]
            sl = h % 2
            if qb == 0 and kt == 0 and h >= 1 and h + 1 < 4:
                load_head(h + 1)
            if i + 2 < NS:
                issue_S(i + 2)
            j = (h * 8 + qb) % 2
            pO, tpO = psb[2 + j], t_psb[2 + j]
            pR, tpR = psb[4 + j], t_psb[4 + j]
            pi = i % NPT
            P.op("pe", lambda e, pO=pO, pi=pi, sl=sl, kt=kt: e.matmul(pO[:, :], VV[sl][:, kt, :], Pt[pi][:],
                                                                   start=(kt == 0), stop=(kt == 31)),
                 r=[t_hd[sl], t_Pt[pi]], w=[tpO])
            P.op("pe", lambda e, pR=pR, pi=pi, kt=kt: e.matmul(pR[:, :], onesbf[:, :], Pt[pi][:],
                                                              start=(kt == 0), stop=(kt == 31)),
                 r=[t_ones, t_Pt[pi]], w=[tpR])
            if kt == 31:
                P.op("dve", lambda e, pR=pR: e.reciprocal(rinv[:], pR[:, :]), r=[tpR], w=[t_rinv])
                P.op("dve", lambda e, pO=pO, j=j: e.tensor_tensor(Ost[j][:], pO[:, :], rinv[:], op=ALU.mult),
                     r=[tpO, t_rinv], w=[t_Ost[j]])
                P.op("sp", lambda e, j=j, h=h, qb=qb: e.dma_start(out=mixT[h, :, qb * 512:(qb + 1) * 512], in_=Ost[j][:]),
                     r=[t_Ost[j]], w=[scr_t["mixT"]], dma=f"bo{j}")
        P.barrier()
    if STOP == "B":
        return finish(nc, P, es, out_d)

    with ExitStack() as esC:
        def sbC(name, shape, dt):
            return esC.enter_context(nc.sbuf_tensor("c_" + name, list(shape), dt))
        kTd = [sbC(f"kTd{i}", [128, T], BF16) for i in range(2)]
        lfd = [sbC(f"lfd{i}", [128, 32, 128], F32) for i in range(2)]
        ktd = [sbC(f"ktd{i}", [128, 32, 128], BF16) for i in range(2)]
        t_dd = P.tls("dd", 2)
        qTh = [sbC(f"qTh{i}", [128, T], BF16) for i in range(2)]
        vhh = [sbC(f"vhh{i}", [128, 32, 128], BF16) for i in range(2)]
        ggh = [sbC(f"ggh{i}", [128, 32, 128], BF16) for i in range(2)]
        t_hh = P.tls("hh", 2)
        of_sb = sbC("of_sb", [128, 32, 128], F32)
        t_of = P.tl("of")
        rT = [sbC(f"rT{i}", [128, T], BF16) for i in range(2)]
        t_rT = P.tls("rT", 2)
        NB = 2
        ebm = [sbC(f"ebm{i}", [128, 256], F32) for i in range(NB)]
        emi = [sbC(f"emi{i}", [128, 128], F32) for i in range(NB)]
        er = [sbC(f"er{i}", [128, 128], F32) for i in range(NB)]
        Wq = [sbC(f"Wq{i}", [128, 192], BF16) for i in range(NB)]
        qi_ = [sbC(f"qi{i}", [128, 128], BF16) for i in range(NB)]
        ki_ = [sbC(f"ki{i}", [128, 128], BF16) for i in range(NB)]
        kd_ = [sbC(f"kd{i}", [128, 128], BF16) for i in range(NB)]
        Am_ = [sbC(f"Am{i}", [128, 128], BF16) for i in range(NB)]
        osum = [sbC(f"osum{i}", [128, 128], F32) for i in range(NB)]
        rr = [sbC(f"rr{i}", [128, 128], BF16) for i in range(NB)]
        t_ebm = P.tls("ebm", NB); t_emi = P.tls("emi", NB); t_er = P.tls("er", NB); t_Wq = P.tls("Wq", NB)
        t_qi = P.tls("qi", NB); t_ki = P.tls("ki", NB); t_kd = P.tls("kd", NB); t_Am = P.tls("Am", NB)
        t_osum = P.tls("osum", NB); t_rr = P.tls("rr", NB)
        S32 = sbC("S32", [128, 128], F32)
        t_S32 = P.tl("S32")
        Sbf = [sbC(f"Sbf{i}", [128, 128], BF16) for i in range(2)]
        t_Sbf = P.tls("Sbf", 2)
        cstat = sbC("cstat", [128, 8], F32)
        t_cstat = P.tls("cstat", 8)
        vm = sbC("vm", [128, 32, 256], BF16)
        t_vm = P.tl("vm")
        cmask = sbC("cmask", [128, 2], F32)
        t_cmask = P.tl("cmask")
        P.op("dve", lambda e: e.memset(cmask[:], 0.0), w=[t_cmask])
        P.op("dve", lambda e: e.memset(cmask[0:64, 0:1], 1.0), w=[t_cmask])
        P.op("dve", lambda e: e.memset(cmask[64:128, 1:2], 1.0), w=[t_cmask])
        for i in range(NB):
            P.op("dve", lambda e, i=i: e.memset(Wq[i][:], 0.0), w=[t_Wq[i]])

        def load_hh(h):
            sl = h % 2
            for (dst, src, tl_) in ((qTh[sl][:], qTs[h, :, :], "qTs"), (vhh[sl][:], vhs[h, :, :, :], "vhs"),
                                    (ggh[sl][:], ggs[h, :, :, :], "ggs")):
                P.op("pool", lambda e, dst=dst, src=src: e.dma_start(out=dst, in_=src), r=[scr_t[tl_]], w=[t_hh[sl]],
                     dma=f"ch{sl}")

        def load_dd(h, d):
            sl = (h * 2 + d) % 2
            for (dst, src, tl_) in ((kTd[sl][:], kTs[d, h, :, :], "kTs"), (lfd[sl][:], lfs[d, h, :, :, :], "lfs"),
                                    (ktd[sl][:], ktms[d, h, :, :, :], "ktms")):
                P.op("pool", lambda e, dst=dst, src=src: e.dma_start(out=dst, in_=src), r=[scr_t[tl_]], w=[t_dd[sl]],
                     dma=f"cd{sl}")

        load_hh(0)
        load_dd(0, 0)
        bcount = [0]
        for h in range(4):
            hs = h % 2
            if h + 1 < 4:
                load_hh(h + 1)
            for c in (0, 1):
                eng_ = "dve" if c == 0 else "pool"
                P.op(eng_, lambda e, c=c, hs=hs: e.tensor_scalar(vm[:, :, c * 128:(c + 1) * 128], vhh[hs][:], cmask[:, c:c + 1],
                                                                 None, op0=ALU.mult),
                     r=[t_hh[hs], t_cmask], w=[t_vm])
            for d in range(2):
                ds_ = (h * 2 + d) % 2
                nd = h * 2 + d + 1
                if nd < 8:
                    load_dd(nd // 2, nd % 2)
                CM = cst32[:, CMF:CMF + 256] if d == 0 else cst32[:, CMB:CMB + 256]
                SU = cst32[:, SUF:SUF + 128] if d == 0 else cst32[:, SUB:SUB + 128]
                MK = cst32[:, MKF:MKF + 128] if d == 0 else cst32[:, MKB:MKB + 128]
                P.op("dve", lambda e: e.memset(S32[:], 0.0), w=[t_S32])
                sb_cur = 0
                P.op("dve", lambda e: e.memset(Sbf[0][:], 0.0), w=[t_Sbf[0]])
                for ib in range(32):
                    blk = ib if d == 0 else 31 - ib
                    corder = (0, 1) if d == 0 else (1, 0)
                    u = bcount[0] % NB
                    bcount[0] += 1
                    lf_b = lfd[ds_][:, blk, :]
                    qT_b = qTh[hs][:, blk * 128:(blk + 1) * 128]
                    kT_b = kTd[ds_][:, blk * 128:(blk + 1) * 128]
                    v_b = vhh[hs][:, blk, :]
                    pA, tpA = getps()
                    P.op("pe", lambda e, pA=pA, lf_b=lf_b, CM=CM: e.matmul(pA[:, 0:256], lf_b, CM, start=True, stop=True),
                         r=[t_dd[ds_], t_cst], w=[tpA])
                    P.op("pe", lambda e, pA=pA, lf_b=lf_b, SU=SU: e.matmul(pA[:, 256:384], SU, lf_b, start=True, stop=True),
                         r=[t_dd[ds_], t_cst], w=[tpA])
                    P.op("act", lambda e, pA=pA, u=u: e.activation(out=ebm[u][:], in_=pA[:, 0:256], func=AF.Exp),
                         r=[tpA], w=[t_ebm[u]])
                    P.op("act", lambda e, pA=pA, u=u: e.activation(out=emi[u][:], in_=pA[:, 128:256], func=AF.Exp, scale=-1.0),
                         r=[tpA], w=[t_emi[u]])
                    P.op("act", lambda e, pA=pA, u=u: e.activation(out=er[u][:], in_=pA[:, 256:384], func=AF.Exp),
                         r=[tpA], w=[t_er[u]])
                    P.op("dve", lambda e, u=u, qT_b=qT_b: e.tensor_tensor(
                        Wq[u][:].rearrange("p (a b) -> p a b", b=64)[:, 0:3:2, :],
                        qT_b.rearrange("p (a b) -> p a b", b=64), ebm[u][:, 0:128].rearrange("p (a b) -> p a b", b=64),
                        op=ALU.mult), r=[t_hh[hs], t_ebm[u]], w=[t_Wq[u]])
                    P.op("pool", lambda e, u=u, qT_b=qT_b: e.tensor_tensor(qi_[u][:], qT_b, ebm[u][:, 128:256], op=ALU.mult),
                         r=[t_hh[hs], t_ebm[u]], w=[t_qi[u]])
                    P.op("pool", lambda e, u=u, kT_b=kT_b: e.tensor_tensor(ki_[u][:], kT_b, emi[u][:], op=ALU.mult),
                         r=[t_dd[ds_], t_emi[u]], w=[t_ki[u]])
                    P.op("dve", lambda e, u=u, blk=blk: e.tensor_tensor(kd_[u][:], ktd[ds_][:, blk, :], er[u][:], op=ALU.mult),
                         r=[t_dd[ds_], t_er[u]], w=[t_kd[u]])
                    p3, tp3 = getps()
                    P.op("pe", lambda e, p3=p3, u=u: e.matmul(p3[:, 0:128], ki_[u][:], qi_[u][:], start=True, stop=True),
                         r=[t_ki[u], t_qi[u]], w=[tp3])
                    P.op("dve", lambda e, p3=p3, u=u, MK=MK: e.tensor_tensor(Am_[u][:], p3[:, 0:128], MK, op=ALU.mult),
                         r=[tp3, t_cst], w=[t_Am[u]])
                    p4, tp4 = getps()
                    P.op("pe", lambda e, p4=p4, u=u, blk=blk: e.matmul(p4[:, 0:256], kd_[u][:], vm[:, blk, :],
                                                                      start=True, stop=True),
                         r=[t_kd[u], t_vm], w=[tp4])
                    p5, tp5 = getps()
                    P.op("pe", lambda e, p5=p5, u=u, v_b=v_b: e.matmul(p5[:, 0:128], Am_[u][:], v_b, start=True, stop=False),
                         r=[t_Am[u], t_hh[hs]], w=[tp5])
                    for ci, c in enumerate(corder):
                        woff = 0 if c == 0 else 64
                        P.op("pe", lambda e, p5=p5, u=u, woff=woff, sbc=sb_cur, ci=ci: e.matmul(
                            p5[:, 0:128], Wq[u][:, woff:woff + 128], Sbf[sbc][:], start=False, stop=(ci == 1)),
                            r=[t_Wq[u], t_Sbf[sb_cur]], w=[tp5])
                        dcol = (c * 64 + 63) if d == 0 else (c * 64)
                        P.op("dve", lambda e, p4=p4, u=u, c=c, dcol=dcol: e.scalar_tensor_tensor(
                            S32[:], S32[:], ebm[u][:, dcol:dcol + 1], p4[:, c * 128:(c + 1) * 128],
                            op0=ALU.mult, op1=ALU.add), r=[t_S32, t_ebm[u], tp4], w=[t_S32])
                        sb_cur = 1 - sb_cur
                        P.op("act", lambda e, sbc=sb_cur: e.activation(out=Sbf[sbc][:], in_=S32[:], func=AF.Copy),
                             r=[t_S32], w=[t_Sbf[sb_cur]])
                    if d == 0:
                        P.op("act", lambda e, p5=p5, blk=blk: e.activation(out=of_sb[:, blk, :], in_=p5[:, 0:128], func=AF.Copy),
                             r=[tp5], w=[t_of])
                    else:
                        sc = cstat[:, u:u + 1]
                        P.op("dve", lambda e, p5=p5, u=u, blk=blk: e.tensor_tensor(osum[u][:], p5[:, 0:128], of_sb[:, blk, :],
                                                                                 op=ALU.add),
                             r=[tp5, t_of], w=[t_osum[u]])
                        P.op("act", lambda e, u=u, sc=sc: e.activation(out=rr[u][:], in_=osum[u][:], func=AF.Square,
                                                                        accum_out=sc),
                             r=[t_osum[u]], w=[t_rr[u], t_cstat[u]])
                        P.op("dve", lambda e, sc=sc: e.tensor_scalar(sc, sc, 1.0 / 128.0, EPS, op0=ALU.mult, op1=ALU.add),
                             r=[t_cstat[u]], w=[t_cstat[u]])
                        act_pow(sc, sc, -0.5, [t_cstat[u]], [t_cstat[u]])
                        P.op("dve", lambda e, u=u, sc=sc, blk=blk: e.scalar_tensor_tensor(
                            rr[u][:], osum[u][:], sc, ggh[hs][:, blk, :], op0=ALU.mult, op1=ALU.mult),
                            r=[t_osum[u], t_cstat[u], t_hh[hs]], w=[t_rr[u]])
                        pt, tpt = getpt()
                        P.op("pe", lambda e, pt=pt, u=u: e.transpose(pt[:, 0:128], rr[u][:], ident), r=[t_rr[u], t_cst], w=[tpt])
                        P.op("act", lambda e, pt=pt, blk=blk: e.activation(out=rT[hs][:, blk * 128:(blk + 1) * 128],
                                                                           in_=pt[:, 0:128], func=AF.Copy),
                             r=[tpt], w=[t_rT[hs]])
            P.op("sp", lambda e, h=h, hs=hs: e.dma_start(out=mixT[4 + h, :, :], in_=rT[hs][:]),
                 r=[t_rT[hs]], w=[scr_t["mixT"]], dma=f"co{hs}")
        P.barrier()
    if STOP == "C":
        return finish(nc, P, es, out_d)

    with ExitStack() as esD:
        def sbD(name, shape, dt):
            return esD.enter_context(nc.sbuf_tensor("d_" + name, list(shape), dt))
        xt = [sbD(f"xt{i}", [128, 4, DM], F32) for i in range(2)]
        t_xt = [P.tls(f"xt{i}_", 4) for i in range(2)]
        mT = [sbD(f"mT{i}", [128, 8, 512], BF16) for i in range(2)]
        t_mT = P.tls("mT", 2)
        hT2 = sbD("hT2", [128, 8, 512], BF16)
        t_hT2 = P.tl("hT2")
        hb2 = [sbD(f"hb2{i}", [128, DM], BF16) for i in range(2)]
        t_hb2 = P.tls("hb2", 2)
        junkD = sbD("junkD", [128, DM], BF16)
        t_junkD = P.tl("junkD")
        actT = sbD("actT", [128, 22, 512], BF16)
        t_actT = P.tl("actT")
        pT = sbD("pT", [128, 2, 512], BF16)
        t_pT = P.tl("pT")
        p32 = [sbD(f"p32{i}", [128, 4, 256], F32) for i in range(2)]
        t_p32 = P.tls("p32", 2)
        pbf = sbD("pbf", [128, 4, 256], BF16)
        t_pbf = P.tl("pbf")
        NW = 8
        Wr = [sbD(f"Wr{i}", [128, 4096], BF16) for i in range(NW)]
        t_Wr = P.tls("Wr", NW)
        tmpD = [sbD(f"tmpD{i}", [128, 512], F32) for i in range(3)]
        t_tmpD = P.tls("tmpD", 3)
        dstat = sbD("dstat", [128, 8], F32)
        t_dstat = P.tls("dstat", 8)
        t_out = P.tl("out")
        P.final_tiles.append(t_out)

        def v3(ap, p=128):
            return ap.rearrange("(c p) n -> p c n", p=p)
        pieces = []

        def mk_pieces(tt):
            L = []
            for half in range(2):
                L.append(("wo", half, v3(wo_s)[:, half * 4:half * 4 + 4, :], (4, 1024)))
            for g in range(6):
                w = 512 if g < 5 else 256
                L.append(("wg", g, v3(wg_s)[:, :, g * 512:g * 512 + w], (8, w)))
                L.append(("wu", g, v3(wu_s)[:, :, g * 512:g * 512 + w], (8, w)))
            for q in range(6):
                n = 4 if q < 5 else 2
                L.append(("wd", q, v3(wd_s)[:, q * 4:q * 4 + n, :], (n, 1024)))
            for half in range(2):
                L.append(("wpg", half, v3(wpg_s)[:, half * 4:half * 4 + 4, :], (4, 1024)))
            L.append(("wpp", 0, v3(wpp_s)[:, :, :], (2, 1024)))
            return L
        allp = []
        for tt in range(NT):
            allp += mk_pieces(tt)
        wstate = dict(issued=0, used=0, done=0)

        def wview(slot, shp):
            a, b = shp
            return Wr[slot][:, 0:a * b].rearrange("p (a b) -> p a b", b=b)

        def issue_w(upto):
            while wstate["issued"] < min(upto, len(allp)):
                i = wstate["issued"]
                slot = i % NW
                _, _, src, shp = allp[i]
                P.op("pool", lambda e, slot=slot, src=src, shp=shp: e.dma_start(out=wview(slot, shp), in_=src),
                     r=[scr_t["wD"]], w=[t_Wr[slot]], dma=f"wr{slot}")
                wstate["issued"] += 1

        def next_w(kind):
            i = wstate["used"]
            assert allp[i][0] == kind, (allp[i][0], kind)
            wstate["used"] += 1
            assert i < wstate["issued"], (i, wstate)
            slot = i % NW
            return wview(slot, allp[i][3]), t_Wr[slot]

        def load_tile(tt):
            sl = tt % 2
            P.op("pool", lambda e, sl=sl, tt=tt: e.dma_start(out=xt[sl][:], in_=x_d[tt * 512:(tt + 1) * 512, :]
                                                          .rearrange("(s p) n -> p s n", p=128)),
                 w=list(t_xt[sl]), dma=f"dx{sl}")
            P.op("pool", lambda e, sl=sl, tt=tt: e.dma_start(out=mT[sl][:], in_=mixT[:, :, tt * 512:(tt + 1) * 512]
                                                          .rearrange("c p t -> p c t")),
                 r=[scr_t["mixT"]], w=[t_mT[sl]], dma=f"dm{sl}")
            P.op("pool", lambda e, sl=sl, tt=tt: e.dma_start(out=p32[sl][:], in_=p_d[tt * 512:(tt + 1) * 512, :]
                                                          .rearrange("(s p) n -> p s n", p=128)),
                 w=[t_p32[sl]], dma=f"dp{sl}")

        def norm_T(sl):
            for sub in range(4):
                hi_ = sub % 2
                sc = dstat[:, sub:sub + 1]
                P.op("act", lambda e, sub=sub, sc=sc: e.activation(out=junkD[:], in_=xt[sl][:, sub, :], func=AF.Square,
                                                                    accum_out=sc),
                     r=[t_xt[sl][sub]], w=[t_junkD, t_dstat[sub]])
                P.op("dve", lambda e, sc=sc: e.tensor_scalar(sc, sc, 1.0 / DM, EPS, op0=ALU.mult, op1=ALU.add),
                     r=[t_dstat[sub]], w=[t_dstat[sub]])
                act_pow(sc, sc, -0.5, [t_dstat[sub]], [t_dstat[sub]])
                P.op("dve", lambda e, sub=sub, hi_=hi_, sc=sc: e.tensor_scalar(hb2[hi_][:], xt[sl][:, sub, :], sc, None,
                                                                              op0=ALU.mult),
                     r=[t_xt[sl][sub], t_dstat[sub]], w=[t_hb2[hi_]])
                for half in range(2):
                    pt, tpt = getpt()
                    for j in range(4):
                        fc = half * 4 + j
                        P.op("pe", lambda e, pt=pt, hi_=hi_, fc=fc, j=j: e.transpose(
                            pt[:, j * 128:(j + 1) * 128], hb2[hi_][:, fc * 128:(fc + 1) * 128], ident),
                            r=[t_hb2[hi_], t_cst], w=[tpt])
                    dst = hT2[:, half * 4:half * 4 + 4, sub * 128:(sub + 1) * 128]
                    src = pt.rearrange("p (a b) -> p a b", b=128)
                    if half == 0:
                        P.op("act", lambda e, dst=dst, src=src: e.activation(out=dst, in_=src, func=AF.Copy),
                             r=[tpt], w=[t_hT2])
                    else:
                        P.op("dve", lambda e, dst=dst, src=src: e.tensor_copy(dst, src), r=[tpt], w=[t_hT2])

        def w_done(n):
            wstate["done"] += n
            issue_w(wstate["done"] + NW)

        load_tile(0)
        issue_w(NW)
        for tt in range(NT):
            sl = tt % 2
            if tt + 1 < NT:
                load_tile(tt + 1)
            wo_p = [next_w("wo"), next_w("wo")]
            for sub in range(4):
                for nch in range(2):
                    pz, tpz = getps()
                    for fc in range(8):
                        wv, twv = wo_p[fc // 4]
                        P.op("pe", lambda e, pz=pz, fc=fc, wv=wv, sub=sub, nch=nch: e.matmul(
                            pz[:, :], mT[sl][:, fc, sub * 128:(sub + 1) * 128], wv[:, fc % 4, nch * 512:(nch + 1) * 512],
                            start=(fc == 0), stop=(fc == 7)), r=[t_mT[sl], twv], w=[tpz])
                    xs_ = xt[sl][:, sub, nch * 512:(nch + 1) * 512]
                    P.op("dve", lambda e, pz=pz, xs_=xs_: e.tensor_tensor(xs_, pz[:, :], xs_, op=ALU.add),
                         r=[tpz, t_xt[sl][sub]], w=[t_xt[sl][sub]])
            w_done(2)
            norm_T(sl)
            for g in range(6):
                nbk = 4 if g < 5 else 2
                wgv, twg = next_w("wg")
                wuv, twu = next_w("wu")
                for nb in range(nbk):
                    jj = g * 4 + nb
                    pg, tpg = getps()
                    pu, tpu = getps()
                    for fc in range(8):
                        P.op("pe", lambda e, pg=pg, fc=fc, wgv=wgv, nb=nb: e.matmul(
                            pg[:, :], wgv[:, fc, nb * 128:(nb + 1) * 128], hT2[:, fc, :], start=(fc == 0), stop=(fc == 7)),
                            r=[twg, t_hT2], w=[tpg])
                    for fc in range(8):
                        P.op("pe", lambda e, pu=pu, fc=fc, wuv=wuv, nb=nb: e.matmul(
                            pu[:, :], wuv[:, fc, nb * 128:(nb + 1) * 128], hT2[:, fc, :], start=(fc == 0), stop=(fc == 7)),
                            r=[twu, t_hT2], w=[tpu])
                    tm = tmpD[jj % 2]
                    ttm = t_tmpD[jj % 2]
                    P.op("act", lambda e, pg=pg, tm=tm: e.activation(out=tm[:], in_=pg[:, :], func=AF.Silu),
                         r=[tpg], w=[ttm])
                    P.op("dve", lambda e, pu=pu, tm=tm, jj=jj: e.tensor_tensor(actT[:, jj, :], tm[:], pu[:, :], op=ALU.mult),
                         r=[tpu, ttm], w=[t_actT])
                w_done(2)
            wd_p = [next_w("wd") for _ in range(6)]
            for sub in range(4):
                for nch in range(2):
                    pz, tpz = getps()
                    for kc in range(22):
                        wv, twv = wd_p[kc // 4]
                        P.op("pe", lambda e, pz=pz, kc=kc, wv=wv, sub=sub, nch=nch: e.matmul(
                            pz[:, :], actT[:, kc, sub * 128:(sub + 1) * 128], wv[:, kc % 4, nch * 512:(nch + 1) * 512],
                            start=(kc == 0), stop=(kc == 21)), r=[t_actT, twv], w=[tpz])
                    xs_ = xt[sl][:, sub, nch * 512:(nch + 1) * 512]
                    P.op("dve", lambda e, pz=pz, xs_=xs_: e.tensor_tensor(xs_, pz[:, :], xs_, op=ALU.add),
                         r=[tpz, t_xt[sl][sub]], w=[t_xt[sl][sub]])
            w_done(6)
            norm_T(sl)
            P.op("dve", lambda e: e.tensor_copy(pbf[:], p32[sl][:]), r=[t_p32[sl]], w=[t_pbf])
            for sub in range(4):
                pt, tpt = getpt()
                for kc in range(2):
                    P.op("pe", lambda e, pt=pt, sub=sub, kc=kc: e.transpose(
                        pt[:, kc * 128:(kc + 1) * 128], pbf[:, sub, kc * 128:(kc + 1) * 128], ident),
                        r=[t_pbf, t_cst], w=[tpt])
                P.op("act", lambda e, pt=pt, sub=sub: e.activation(
                    out=pT[:, :, sub * 128:(sub + 1) * 128], in_=pt[:, 0:256].rearrange("p (a b) -> p a b", b=128),
                    func=AF.Copy), r=[tpt], w=[t_pT])
            wpg_p = [next_w("wpg"), next_w("wpg")]
            wpp_v, twpp = next_w("wpp")
            for sub in range(4):
                for nch in range(2):
                    pG, tpG = getps()
                    pP, tpP = getps()
                    for fc in range(8):
                        wv, twv = wpg_p[fc // 4]
                        P.op("pe", lambda e, pG=pG, fc=fc, wv=wv, sub=sub, nch=nch: e.matmul(
                            pG[:, :], hT2[:, fc, sub * 128:(sub + 1) * 128], wv[:, fc % 4, nch * 512:(nch + 1) * 512],
                            start=(fc == 0), stop=(fc == 7)), r=[t_hT2, twv], w=[tpG])
                    for kc in range(2):
                        P.op("pe", lambda e, pP=pP, kc=kc, sub=sub, nch=nch: e.matmul(
                            pP[:, :], pT[:, kc, sub * 128:(sub + 1) * 128], wpp_v[:, kc, nch * 512:(nch + 1) * 512],
                            start=(kc == 0), stop=(kc == 1)), r=[t_pT, twpp], w=[tpP])
                    tm = tmpD[nch]
                    ttm = t_tmpD[nch]
                    P.op("act", lambda e, pG=pG, tm=tm: e.activation(out=tm[:], in_=pG[:, :], func=AF.Sigmoid),
                         r=[tpG], w=[ttm])
                    P.op("dve", lambda e, pP=pP, tm=tm: e.tensor_tensor(tm[:], tm[:], pP[:, :], op=ALU.mult),
                         r=[tpP, ttm], w=[ttm])
                    xs_ = xt[sl][:, sub, nch * 512:(nch + 1) * 512]
                    P.op("pool", lambda e, tm=tm, xs_=xs_: e.tensor_tensor(xs_, xs_, tm[:], op=ALU.add),
                         r=[ttm, t_xt[sl][sub]], w=[t_xt[sl][sub]])
                r0 = tt * 512 + sub * 128
                P.op("sp", lambda e, sub=sub, r0=r0: e.dma_start(out=out_d[r0:r0 + 128, :], in_=xt[sl][:, sub, :]),
                     r=[t_xt[sl][sub]], w=[t_out], dma=f"do{sl}")
            w_done(3)
    return finish(nc, P, es, out_d)


def finish(nc, P, es, out_d):
    t_o = P.tl("outfin")
    if STOP:
        zt = P.zt
        tz = P.tl("zt")
        P.op("dve", lambda e: e.memset(zt[:], 0.0), w=[tz])
        P.op("sp", lambda e: e.dma_start(out=out_d[0:128, :], in_=zt[:]), r=[tz], w=[t_o], dma="ofin")
    P.op("sp", None, r=[t_o] + P.final_tiles)
    P.emit()
    return nc


def _consts():
    idn = np.eye(128, dtype=np.float32)
    s = np.arange(64)[:, None]
    t = np.arange(64)[None, :]

    def bd(m):
        z = np.zeros((128, 128), np.float32)
        z[:64, :64] = m
        z[64:, 64:] = m
        return z
    Uf = (s <= t).astype(np.float32)
    Umf = Uf - Uf[:, 31:32]
    SUf = (s > t).astype(np.float32)
    Ub = (s >= t).astype(np.float32)
    Umb = Ub - Ub[:, 32:33]
    SUb = (s < t).astype(np.float32)
    return np.concatenate([idn, bd(Uf), bd(Umf), bd(SUf), bd(Uf), bd(Ub), bd(Umb), bd(SUb), bd(Ub)], axis=1)


def _layout_inputs(inp):
    f = np.float32
    w_in = np.asarray(inp["w_in"][0], f)
    krot_cols = np.concatenate([np.arange(512 + 32, 512 + 64), np.arange(512, 512 + 32)])
    w_in_x = np.ascontiguousarray(np.concatenate([w_in, w_in[:, krot_cols]], axis=1))
    wqb = np.asarray(inp["w_qb"][0], f)
    nope = np.concatenate([np.arange(h * 192, h * 192 + 128) for h in range(4)])
    rope = np.concatenate([np.arange(h * 192 + 128, h * 192 + 192) for h in range(4)])
    rot = np.concatenate([np.concatenate([np.arange(h * 192 + 160, h * 192 + 192), np.arange(h * 192 + 128, h * 192 + 160)])
                          for h in range(4)])
    wqb_x = np.ascontiguousarray(wqb[:, np.concatenate([nope, rope, rot])])
    wkvb = np.asarray(inp["w_kvb"][0], f)
    kn = np.concatenate([np.arange(h * 256, h * 256 + 128) for h in range(4)])
    vv = np.concatenate([np.arange(h * 256 + 128, h * 256 + 256) for h in range(4)])
    wkvb_x = np.ascontiguousarray(wkvb[:, np.concatenate([kn, vv])])

    sm = np.zeros((128, 64), f)

    def colmajor(v, n):
        return np.asarray(v, f).reshape(n, 128).T
    sm[:, 0:8] = colmajor(inp["g_mix"][0], 8)
    sm[:, 8:16] = colmajor(inp["g_ffn"][0], 8)
    sm[:, 16:24] = colmajor(inp["g_ple"][0], 8)
    sm[:, 24:26] = colmajor(inp["g_qa"][0], 2)
    sm[:, 26:28] = colmajor(inp["g_kva"][0], 2)
    perm = np.concatenate([np.arange(32, 64), np.arange(0, 32)])
    for base, g in ((28, np.asarray(inp["g_qn"][0], f)), (31, np.asarray(inp["g_kn"][0], f))):
        sm[:, base] = g[:128]
        sm[:64, base + 1] = g[128:192]
        sm[:64, base + 2] = g[128:192][perm]
    inv_freq = (10000.0 ** (-np.arange(0, 64, 2, dtype=np.float32) / 64)).astype(f)
    sm[:64, 34] = np.concatenate([inv_freq, inv_freq])
    sm[:, 35] = 1.0
    sm[:32, 35] = -1.0
    lbp = np.asarray(inp["lb_param"], f)
    sm[:, 36:52] = lbp.reshape(2, 2, 4, 128).transpose(3, 0, 1, 2).reshape(128, 16)
    lb_tm = np.ascontiguousarray(np.broadcast_to(lbp.reshape(1, 4, 512), (128, 4, 512)))
    ghg_tm = np.ascontiguousarray(np.broadcast_to(np.asarray(inp["g_hgo"][0], f).reshape(1, 512), (128, 512)))
    shared = dict(w_in=w_in_x, w_qb=wqb_x, w_kvb=wkvb_x,
                  w_o=np.ascontiguousarray(inp["w_o"][0], f), w_gate=np.ascontiguousarray(inp["w_gate"][0], f),
                  w_up=np.ascontiguousarray(inp["w_up"][0], f), w_down=np.ascontiguousarray(inp["w_down"][0], f),
                  w_pg=np.ascontiguousarray(inp["w_ple_gate"][0], f), w_pp=np.ascontiguousarray(inp["w_ple_proj"][0], f),
                  smalls=sm, lb_tm=lb_tm, ghg_tm=ghg_tm, consts=_consts())
    maps = []
    x = np.asarray(inp["x"], f)
    p = np.asarray(inp["p"], f)
    pos = np.asarray(inp["positions"]).astype(np.int32)
    for b in range(8):
        m = dict(shared)
        m["x"] = np.ascontiguousarray(x[b])
        m["p"] = np.ascontiguousarray(p[0, b])
        m["pos"] = np.ascontiguousarray(np.broadcast_to(pos[b][None, :], (64, T)))
        maps.append(m)
    return maps


def kernel(**inputs):
    maps = _layout_inputs(inputs)
    dbg = tuple(DEBUG.split(",")) if DEBUG else ()
    nc = build_program(dbg)
    ncores = int(os.environ.get("MK_CORES", 8))
    if STOP:
        maps = [{k: v for k, v in m.items() if k != "p"} for m in maps]
    maps = maps[:ncores]
    res = run_bass_kernel_spmd(nc, maps, core_ids=list(range(ncores)))
    if dbg:
        kernel.debug = res.results
    out = np.stack([np.asarray(r["out"], np.float32) for r in res.results], axis=0)
    if ncores < 8:
        out = np.concatenate([out, np.zeros((8 - ncores,) + out.shape[1:], np.float32)], axis=0)
    return out
```
